# Optimizing a Trainium2 kernel written in Bass

```python
import jax, jax.numpy as jnp
from jax import lax
import numpy as np

D_MODEL = 1024
BATCH = 2
SEQ = 16384
DEPTH = 2

HGRN_WIDTH = D_MODEL // 2
HGRN_EXPAND = 128
HGRN_HEADS = HGRN_WIDTH // HGRN_EXPAND
HGRN_DK = HGRN_EXPAND
HGRN_DV = HGRN_WIDTH // HGRN_HEADS
HGRN_CHUNK = 64
ATTN_WIDTH = D_MODEL - HGRN_WIDTH
ATTN_HEADS = 8
ATTN_DH = ATTN_WIDTH // ATTN_HEADS
DILATED_PATTERNS = ((128, 1), (512, 4), (2048, 16))
D_FF = 4 * D_MODEL
IN_SIZES = (HGRN_WIDTH, HGRN_WIDTH, HGRN_WIDTH, HGRN_WIDTH, ATTN_WIDTH, ATTN_WIDTH, ATTN_WIDTH)
IN_COLS = sum(IN_SIZES)
IN_SPLITS = tuple(int(c) for c in np.cumsum(IN_SIZES)[:-1])
DEEPNORM_ALPHA = (2 * DEPTH) ** 0.25
DEEPNORM_BETA = (8 * DEPTH) ** -0.25
LN_EPS = 1e-5
RMS_EPS = 1e-6

kernel_name = "hymba_style_hgrn2_dilated_attn_deepnorm"


def layer_norm(x, g, b):
    xf = x.astype(jnp.float32)
    mu = jnp.mean(xf, axis=-1, keepdims=True)
    var = jnp.mean(jnp.square(xf - mu), axis=-1, keepdims=True)
    y = (xf - mu) * lax.rsqrt(var + LN_EPS) * g.astype(jnp.float32) + b.astype(jnp.float32)
    return y.astype(x.dtype)


def hgrn2_mixer(q, f_logit, i, gate, lb, norm_w):
    B, S, _ = q.shape
    H, dk, dv, C = HGRN_HEADS, HGRN_DK, HGRN_DV, HGRN_CHUNK
    nc = S // C
    f32 = jnp.float32
    q = jax.nn.silu(q.astype(f32))
    lb = lb.astype(f32)
    log_f = jnp.logaddexp(jnp.log(lb), jnp.log1p(-lb) + jax.nn.log_sigmoid(f_logit.astype(f32)))
    k = -jnp.expm1(log_f)
    v = i.astype(f32)

    def to_chunks(t, d):
        return t.reshape(B, nc, C, H, d).transpose(1, 0, 3, 2, 4)

    tri = jnp.tril(jnp.ones((C, C), dtype=bool))[:, :, None]

    def step(state, inp):
        qc, kc, vc, gc = inp
        G = jnp.cumsum(gc, axis=2)
        o_inter = jnp.einsum('bhtk,bhkv->bhtv', qc * jnp.exp(G), state)
        diff = G[:, :, :, None, :] - G[:, :, None, :, :]
        decay = jnp.exp(jnp.where(tri, diff, -jnp.inf))
        scores = jnp.einsum('bhtk,bhsk,bhtsk->bhts', qc, kc, decay)
        o_intra = jnp.einsum('bhts,bhsv->bhtv', scores, vc)
        G_last = G[:, :, -1:, :]
        k_dec = kc * jnp.exp(G_last - G)
        new_state = jnp.exp(G_last[:, :, 0, :])[..., None] * state + jnp.einsum('bhsk,bhsv->bhkv', k_dec, vc)
        return new_state, o_inter + o_intra

    state0 = jnp.zeros((B, H, dk, dv), f32)
    _, o = lax.scan(step, state0, (to_chunks(q, dk), to_chunks(k, dk), to_chunks(v, dv), to_chunks(log_f, dk)))
    o = o.transpose(1, 0, 3, 2, 4).reshape(B, S, H, dv)
    o = o * lax.rsqrt(jnp.mean(o * o, axis=-1, keepdims=True) + RMS_EPS) * norm_w.astype(f32).reshape(H, dv)
    return o.reshape(B, S, H * dv) * jax.nn.sigmoid(gate.astype(f32))


def dilated_branch(q, k, v, window, dilation):
    B, S, H, dh = q.shape
    L = S // dilation
    blk = window // dilation
    nb = -(-L // blk)
    Lp = nb * blk

    def sub(t):
        t = t.reshape(B, L, dilation, H, dh).transpose(0, 3, 2, 1, 4)
        t = jnp.pad(t, ((0, 0), (0, 0), (0, 0), (0, Lp - L), (0, 0)))
        return t.reshape(B, H, dilation, nb, blk, dh)

    qb, kb, vb = sub(q), sub(k), sub(v)

    def with_prev(t):
        prev = jnp.pad(t, ((0, 0), (0, 0), (0, 0), (1, 0), (0, 0), (0, 0)))[:, :, :, :-1]
        return jnp.concatenate([prev, t], axis=4)

    kc, vc = with_prev(kb), with_prev(vb)
    s = jnp.einsum('bhrnqd,bhrnkd->bhrnqk', qb, kc)
    qpos = jnp.arange(blk)[:, None] + blk
    kpos = jnp.arange(2 * blk)[None, :]
    dist = qpos - kpos
    band = (dist >= 0) & (dist <= blk)
    first_ok = (jnp.arange(nb)[:, None, None] > 0) | (kpos[None] >= blk)
    mask = band[None] & first_ok
    s = jnp.where(mask, s, -jnp.inf)
    m = jnp.max(s, axis=-1, keepdims=True)
    p = jnp.exp(s - m)
    denom = jnp.sum(p, axis=-1, keepdims=True)
    o = jnp.einsum('bhrnqk,bhrnkd->bhrnqd', p, vc) / denom

    def unsub(t):
        c = t.shape[-1]
        t = t.reshape(B, H, dilation, Lp, c)[:, :, :, :L]
        return t.transpose(0, 3, 2, 1, 4).reshape(B, S, H, c)

    return unsub(o), unsub(m), unsub(denom)


def dilated_attention(q, k, v):
    B, S, _ = q.shape
    f32 = jnp.float32
    q = q.astype(f32).reshape(B, S, ATTN_HEADS, ATTN_DH) * (ATTN_DH ** -0.5)
    k = k.astype(f32).reshape(B, S, ATTN_HEADS, ATTN_DH)
    v = v.astype(f32).reshape(B, S, ATTN_HEADS, ATTN_DH)
    branches = [dilated_branch(q, k, v, w, d) for (w, d) in DILATED_PATTERNS]
    m_all = jnp.max(jnp.stack([b[1] for b in branches]), axis=0)
    weights = [b[2] * jnp.exp(b[1] - m_all) for b in branches]
    num = sum(wt * b[0] for wt, b in zip(weights, branches))
    o = num / sum(weights)
    return o.reshape(B, S, ATTN_WIDTH)


def hybrid_mixer(x, w_in, w_out, lb, hgrn_norm_w):
    proj = jnp.einsum('bsd,dn->bsn', x, w_in)
    hq, hf, hi, hg, aq, ak, av = jnp.split(proj, IN_SPLITS, axis=-1)
    o_rec = hgrn2_mixer(hq, hf, hi, hg, lb, hgrn_norm_w)
    o_att = dilated_attention(aq, ak, av)
    o = jnp.concatenate([o_rec, o_att], axis=-1).astype(x.dtype)
    return jnp.einsum('bsn,nd->bsd', o, w_out)


def sq_relu_mlp(x, w1, w2):
    h = jnp.square(jax.nn.relu(jnp.einsum('bsd,df->bsf', x, w1)))
    return jnp.einsum('bsf,fd->bsd', h, w2)


def setup_inputs(seed: int = 0) -> dict:
    key = jax.random.key(seed)
    ks = jax.random.split(key, 12)
    f32 = jnp.float32
    x = jax.random.normal(ks[0], (BATCH, SEQ, D_MODEL), f32)
    w_in = jax.random.normal(ks[1], (DEPTH, D_MODEL, IN_COLS), f32) * D_MODEL ** -0.5
    w_out = jax.random.normal(ks[2], (DEPTH, D_MODEL, D_MODEL), f32) * (D_MODEL ** -0.5 * DEEPNORM_BETA)
    lower_bounds = 0.1 * jax.random.normal(ks[3], (DEPTH, HGRN_WIDTH), f32)
    hgrn_norm_w = 1.0 + 0.02 * jax.random.normal(ks[4], (DEPTH, HGRN_WIDTH), f32)
    ln1_g = 1.0 + 0.02 * jax.random.normal(ks[5], (DEPTH, D_MODEL), f32)
    ln1_b = 0.02 * jax.random.normal(ks[6], (DEPTH, D_MODEL), f32)
    w_ff1 = jax.random.normal(ks[7], (DEPTH, D_MODEL, D_FF), f32) * D_MODEL ** -0.5
    w_ff2 = jax.random.normal(ks[8], (DEPTH, D_FF, D_MODEL), f32) * (D_FF ** -0.5 * DEEPNORM_BETA)
    ln2_g = 1.0 + 0.02 * jax.random.normal(ks[9], (DEPTH, D_MODEL), f32)
    ln2_b = 0.02 * jax.random.normal(ks[10], (DEPTH, D_MODEL), f32)
    return {"x": x, "w_in": w_in, "w_out": w_out, "lower_bounds": lower_bounds,
            "hgrn_norm_w": hgrn_norm_w, "ln1_g": ln1_g, "ln1_b": ln1_b,
            "w_ff1": w_ff1, "w_ff2": w_ff2, "ln2_g": ln2_g, "ln2_b": ln2_b}


def reference(x, w_in, w_out, lower_bounds, hgrn_norm_w, ln1_g, ln1_b, w_ff1, w_ff2, ln2_g, ln2_b):
    lbs = jnp.cumsum(jax.nn.softmax(lower_bounds.astype(jnp.float32), axis=0), axis=0)
    lbs = lbs - lbs[0:1]
    for l in range(DEPTH):
        y = hybrid_mixer(x, w_in[l], w_out[l], lbs[l], hgrn_norm_w[l])
        x = layer_norm(DEEPNORM_ALPHA * x + y, ln1_g[l], ln1_b[l])
        y = sq_relu_mlp(x, w_ff1[l], w_ff2[l])
        x = layer_norm(DEEPNORM_ALPHA * x + y, ln2_g[l], ln2_b[l])
    return x
```

```python
import numpy as np
import ml_dtypes
from contextlib import ExitStack
import concourse.bass as bass
import concourse.mybir as mybir
from concourse.bass_utils import run_bass_kernel_spmd

F32 = mybir.dt.float32
BF16 = mybir.dt.bfloat16
AF = mybir.ActivationFunctionType
ALU = mybir.AluOpType

D = 1024
T = 512
NBLK = T // 128
NCH = T // 64
NSLOT = 20
NDELTA = 17
NMASK = NDELTA + 2 * (NBLK - 1)
ALPHA = 4.0 ** 0.25
LN_EPS = 1e-5
RMS_EPS = 1e-6
REF = 31


class Sync:
    EPOCH = 30000
    NDMA = 24
    NHW = 16

    def __init__(self, nc, es):
        self.nc, self.es = nc, es
        self.eng = {"pe": nc.tensor, "act": nc.scalar, "dve": nc.vector, "pool": nc.gpsimd, "sp": nc.sync}
        self.cnt = {e: 0 for e in self.eng}
        self.sems = {e: [] for e in self.eng}
        self.waited = {e: {} for e in self.eng}
        self.lastw = {}
        self.readers = {}
        self.dsem = [es.enter_context(nc.semaphore("dma%d" % i)) for i in range(self.NDMA)]
        self.dcnt = [0] * self.NDMA
        self.dnext = 0
        self.dnext_sw = 0
        self.nwaits = 0

    def _esem(self, e, ep):
        while len(self.sems[e]) <= ep:
            self.sems[e].append(self.es.enter_context(self.nc.semaphore("c_%s_%d" % (e, len(self.sems[e])))))
        return self.sems[e][ep]

    def _wait(self, eng, key, n):
        if n <= 0:
            return
        if self.waited[eng].get(key, 0) >= n:
            return
        self.waited[eng][key] = n
        self.nwaits += 1
        if key[0] == "e":
            ep = (n - 1) // self.EPOCH
            self.eng[eng].wait_ge(self._esem(key[1], ep), n - ep * self.EPOCH)
        else:
            self.eng[eng].wait_ge(self.dsem[key[1]], n)

    def _deps(self, eng, r, w):
        deps = {}

        def add(kn):
            if kn is None:
                return
            k, n = kn
            if k == ("e", "pe") and eng == "pe":
                return
            if deps.get(k, 0) < n:
                deps[k] = n
        for x in r:
            add(self.lastw.get(x))
        for x in w:
            add(self.lastw.get(x))
            for kn in self.readers.get(x, {}).items():
                add(kn)
        for k, n in deps.items():
            self._wait(eng, k, n)

    def _record(self, key, n, r, w):
        for x in r:
            self.readers.setdefault(x, {})[key] = n
        for x in w:
            self.lastw[x] = (key, n)
            self.readers[x] = {}

    def op(self, eng, fn, r=(), w=()):
        self._deps(eng, r, w)
        ins = fn(self.eng[eng])
        self.cnt[eng] += 1
        n = self.cnt[eng]
        ep = (n - 1) // self.EPOCH
        ins.then_inc(self._esem(eng, ep), 1)
        self._record(("e", eng), n, r, w)
        return ins

    def dma(self, q, out, in_, r=(), w=()):
        if q == "pool":
            s = self.NHW + self.dnext_sw
            self.dnext_sw = (self.dnext_sw + 1) % (self.NDMA - self.NHW)
        else:
            s = self.dnext
            self.dnext = (self.dnext + 1) % self.NHW
        self._wait(q, ("d", s), self.dcnt[s])
        self._deps(q, r, w)
        ins = self.eng[q].dma_start(out=out, in_=in_)
        self.dcnt[s] += 16
        ins.then_inc(self.dsem[s], 16)
        self._record(("d", s), self.dcnt[s], r, w)

    def barrier(self):
        for e in self.eng:
            for x in self.eng:
                if x != e:
                    self._wait(e, ("e", x), self.cnt[x])
            for s in range(self.NDMA):
                self._wait(e, ("d", s), self.dcnt[s])

    def finish(self, eng="sp"):
        for x in self.eng:
            if x != eng:
                self._wait(eng, ("e", x), self.cnt[x])
        for s in range(self.NDMA):
            self._wait(eng, ("d", s), self.dcnt[s])


def host_consts():
    ident = np.eye(128, dtype=np.float32).astype(ml_dtypes.bfloat16)
    s = np.arange(64)
    tri64 = (s[:, None] <= s[None, :]).astype(np.float32)
    tri = np.zeros((128, 128), np.float32)
    tri[0:64, 0:64] = tri64
    tri[64:128, 64:128] = tri64
    tri = tri.astype(ml_dtypes.bfloat16)
    j = np.arange(128)[:, None]
    i = np.arange(128)[None, :]
    am = np.zeros((128, NMASK, 128), np.float32)
    for a in range(NMASK):
        dist = 128 * (a - (NBLK - 1)) + i - j
        for (win, dil) in ((128, 1), (512, 4), (2048, 16)):
            am[:, a, :] += ((dist >= 0) & (dist <= win) & (dist % dil == 0)).astype(np.float32)
    am = am.astype(ml_dtypes.bfloat16)
    reset = np.ones((128, T), np.float32)
    reset[:, ::64] = 0.0
    return {"c_ident": ident, "c_tri": tri, "c_amask": am, "c_reset": reset}


class Builder:
    def __init__(self, ntok_in, layer_cfgs, ntok_out, debug=None):
        self.nc = nc = bass.Bass("TRN2", target_bir_lowering=False)
        self.es = ExitStack()
        self.ntok_in, self.ntok_out = ntok_in, ntok_out
        dt = nc.dram_tensor
        self.xT = dt("xT", [D, ntok_in], F32, kind="ExternalInput").ap()
        self.w_in = dt("w_in", [2, D, 3584], F32, kind="ExternalInput").ap()
        self.w_out = dt("w_out", [2, D, D], F32, kind="ExternalInput").ap()
        self.w_ff1 = dt("w_ff1", [2, D, 4096], F32, kind="ExternalInput").ap()
        self.w_ff2 = dt("w_ff2", [2, 4096, D], F32, kind="ExternalInput").ap()
        self.vecs = {}
        for nm, n in (("lower_bounds", 512), ("hgrn_norm_w", 512), ("ln1_g", D), ("ln1_b", D),
                      ("ln2_g", D), ("ln2_b", D)):
            self.vecs[nm] = dt(nm, [2, 128, n // 128], F32, kind="ExternalInput").ap()
        self.flag_d = dt("flag", [128, 1], F32, kind="ExternalInput").ap()
        self.c_ident = dt("c_ident", [128, 128], BF16, kind="ExternalInput").ap()
        self.c_tri = dt("c_tri", [128, 128], BF16, kind="ExternalInput").ap()
        self.c_amask = dt("c_amask", [128, NMASK, 128], BF16, kind="ExternalInput").ap()
        self.c_reset = dt("c_reset", [128, T], F32, kind="ExternalInput").ap()
        self.outT = dt("outT", [D, ntok_out], F32, kind="ExternalOutput").ap()
        self.layer_cfgs = layer_cfgs
        self.debug = debug

    def sb(self, name, shape, dtype, es=None):
        return (es or self.es).enter_context(self.nc.sbuf_tensor(name, shape, dtype))

    def build(self):
        nc, es = self.nc, self.es
        with es:
            self.S = S = Sync(nc, es)
            self.setup_consts()
            self.ps = [es.enter_context(nc.psum_tensor("ps%d" % i, [128, 512], F32)) for i in range(8)]
            nlay = len(self.layer_cfgs)
            src = self.xT.rearrange("(c p) t -> p c t", p=128)
            for li, cfg in enumerate(self.layer_cfgs):
                nfull = cfg["ntiles"] - cfg["kv_only"]
                x1a = nc.dram_tensor("x1a_%d" % li, [128, 8, nfull * T], F32, kind="Internal").ap()
                if li == nlay - 1:
                    dst = self.outT.rearrange("(c p) t -> p c t", p=128)
                else:
                    dst = nc.dram_tensor("xmid_%d" % li, [128, 8, nfull * T], F32, kind="Internal").ap()
                self.phase_a(cfg, src, x1a)
                S.barrier()
                self.phase_b(cfg, x1a, dst, nfull)
                S.barrier()
                src = dst
            S.finish("sp")
        return nc

    def setup_consts(self):
        nc, S = self.nc, None
        S = self.S
        self.ident = self.sb("ident", [128, 128], BF16)
        self.tri = self.sb("tri", [128, 128], BF16)
        self.amask = self.sb("amask", [128, NMASK, 128], BF16)
        self.reset = self.sb("reset", [128, T], F32)
        self.flag = self.sb("flag_sb", [128, 1], F32)
        self.avgb = self.sb("avgb", [128, 128], BF16)
        self.avg128 = self.sb("avg128", [128, 128], F32)
        S.dma("sp", self.ident[:], self.c_ident, w=["ident"])
        S.dma("sp", self.tri[:], self.c_tri, w=["tri"])
        S.dma("sp", self.amask[:], self.c_amask, w=["amask"])
        S.dma("sp", self.reset[:], self.c_reset, w=["reset"])
        S.dma("sp", self.flag[:], self.flag_d, w=["flag"])
        S.op("dve", lambda e: e.memset(self.avgb[:], 1.0 / 1024), w=["avgb"])
        S.op("dve", lambda e: e.memset(self.avg128[:], 1.0 / 128), w=["avg128"])
        self.vsb = {}
        for nm, ap in self.vecs.items():
            n = ap.shape[2]
            t = self.sb("v_" + nm, [128, 2, n], F32)
            for l in range(2):
                S.dma("sp", t[:, l, :], ap[l], w=["v_" + nm])
            self.vsb[nm] = t
        lbt = self.vsb["lower_bounds"]
        e = self.sb("lb_e", [128, 2, 4], F32)
        self.lb = self.sb("lb", [128, 2, 4], F32)
        self.oml = self.sb("oml", [128, 2, 4], F32)
        ssum = self.sb("lb_s", [128, 4], F32)
        S.op("act", lambda en: en.activation(out=e[:], in_=lbt[:], func=AF.Exp), r=["v_lower_bounds"], w=["lb_e"])
        S.op("dve", lambda en: en.tensor_tensor(out=ssum[:], in0=e[:, 0, :], in1=e[:, 1, :], op=ALU.add),
             r=["lb_e"], w=["lb_s"])
        S.op("dve", lambda en: en.reciprocal(out=ssum[:], in_=ssum[:]), r=["lb_s"], w=["lb_s"])
        S.op("dve", lambda en: en.tensor_tensor(out=self.lb[:, 0, :], in0=e[:, 0, :], in1=e[:, 0, :], op=ALU.subtract),
             r=["lb_e", "lb_s"], w=["lb"])
        S.op("dve", lambda en: en.tensor_tensor(out=self.lb[:, 1, :], in0=e[:, 1, :], in1=ssum[:], op=ALU.mult),
             r=["lb_e", "lb_s", "lb"], w=["lb"])
        S.op("dve", lambda en: en.tensor_scalar(out=self.oml[:], in0=self.lb[:], scalar1=-1.0, scalar2=1.0,
                                                op0=ALU.mult, op1=ALU.add), r=["lb"], w=["oml"])

    def layer_norm(self, zbuf, zres, emit_chunk):
        S = self.S
        psm, psq = self.ps[0], self.ps[1]
        for m in range(8):
            zs, zc = self.ln_zs[m % 2], self.ln_zc[m % 2]
            S.op("act", lambda e: e.activation(out=zs[:], in_=zbuf[:, m, :], func=AF.Square),
                 r=[zres(m)], w=[("lnzs", m % 2)])
            S.op("dve", lambda e: e.tensor_copy(out=zc[:], in_=zbuf[:, m, :]), r=[zres(m)], w=[("lnzc", m % 2)])
            S.op("pe", lambda e: e.matmul(psm[:, 0:T], self.avgb[:], zc[:], start=(m == 0), stop=(m == 7)),
                 r=[("lnzc", m % 2), "avgb"], w=["ps0"])
            S.op("pe", lambda e: e.matmul(psq[:, 0:T], self.avgb[:], zs[:], start=(m == 0), stop=(m == 7)),
                 r=[("lnzs", m % 2), "avgb"], w=["ps1"])
        mean, rstd = self.ln_mean, self.ln_rstd
        S.op("act", lambda e: e.activation(out=mean[:], in_=psm[:, 0:T], func=AF.Copy), w=["ps0", "ln_mean"])
        S.op("act", lambda e: e.activation(out=rstd[:], in_=psm[:, 0:T], func=AF.Square), w=["ps0", "ln_rstd"])
        S.op("dve", lambda e: e.tensor_tensor(out=rstd[:], in0=psq[:, 0:T], in1=rstd[:], op=ALU.subtract),
             w=["ps1", "ln_rstd"])
        S.op("dve", lambda e: e.tensor_scalar(out=rstd[:], in0=rstd[:], scalar1=LN_EPS, scalar2=None, op0=ALU.add),
             w=["ln_rstd"])
        S.op("act", lambda e: e.activation(out=rstd[:], in_=rstd[:], func=AF.Ln), w=["ln_rstd"])
        S.op("act", lambda e: e.activation(out=rstd[:], in_=rstd[:], func=AF.Exp, scale=-0.5), w=["ln_rstd"])
        for m in range(8):
            S.op("dve", lambda e: e.tensor_tensor(out=zbuf[:, m, :], in0=zbuf[:, m, :], in1=mean[:], op=ALU.subtract),
                 r=["ln_mean"], w=[zres(m)])
            S.op("dve", lambda e: e.tensor_tensor(out=zbuf[:, m, :], in0=zbuf[:, m, :], in1=rstd[:], op=ALU.mult),
                 r=["ln_rstd"], w=[zres(m)])
            emit_chunk(m)

    def phase_a(self, cfg, src, x1a):
        nc, S = self.nc, self.S
        l = cfg["l"]
        with ExitStack() as es:
            sb = lambda n, s, d: self.sb(n + "_a%d" % l, s, d, es)
            Win = sb("Win", [128, 8, 3584], BF16)
            Wo = sb("Wo", [128, 8, D], BF16)
            for k in range(8):
                S.dma("pool", Win[:, k, :], self.w_in[l, k * 128:(k + 1) * 128, :], w=[("Win", k)])
            for k in range(8):
                S.dma("pool", Wo[:, k, :], self.w_out[l, k * 128:(k + 1) * 128, :], w=["Wo"])
            NXB = 1
            xTb = [sb("xTb%d" % i, [128, 8, T], BF16) for i in range(NXB)]
            xr = [sb("xr%d" % i, [128, T], F32) for i in range(2)]
            zr = [sb("zr%d" % i, [128, T], F32) for i in range(2)]
            hf = {n: sb("h_" + n, [128, T], F32) for n in ("A", "Q", "B", "Kk", "G", "Dm", "Ex")}
            tmpU = sb("tmpU", [128, 128], F32)
            qt2 = [sb("qt%d" % i, [128, T], BF16) for i in range(2)]
            Sg2 = [sb("Sg%d" % i, [128, T], F32) for i in range(2)]
            T1 = sb("T1", [128, T], F32)
            kt = sb("kt", [128, T], BF16)
            ktokE = sb("ktokE", [128, NBLK, 128], BF16)
            ktokO = sb("ktokO", [128, NBLK, 128], BF16)
            ATs = sb("ATs", [128, T], BF16)
            vtok = sb("vtok", [128, NBLK, 512], BF16)
            Sp = sb("Sp", [128, NCH, 128], BF16)
            St = sb("St", [128, 4, 128], F32)
            sc = {n: sb("sc_" + n, [128, NCH], F32) for n in ("dl", "ea", "eb", "er")}
            oT = sb("oT", [128, 8, T], BF16)
            qZ = sb("qZ", [128, 8, T], BF16)
            Kc = sb("Kc", [128, 4, NSLOT * 128], BF16)
            Vc = sb("Vc", [128, NSLOT, 4, 192], BF16)
            Eb = [sb("Eb%d" % i, [128, 512], BF16) for i in range(4)]
            rden = [sb("rden%d" % i, [128, T], F32) for i in range(2)]

            S.op("dve", lambda e: e.memset(St[:], 0.0), w=["St"])
            S.op("dve", lambda e: e.memset(ktokE[:], 0.0), w=["ktok"])
            S.op("dve", lambda e: e.memset(ktokO[:], 0.0), w=["ktok"])
            S.op("dve", lambda e: e.memset(qZ[:], 0.0), w=[("qZ", p) for p in range(4)])
            S.op("dve", lambda e: e.memset(Kc[:], 0.0), w=[("kc", s) for s in range(NSLOT)])
            S.op("dve", lambda e: e.memset(Vc[:], 0.0), w=[("vc", s) for s in range(NSLOT)])

            lbv, omlv = self.lb[:, l, :], self.oml[:, l, :]
            nw = self.vsb["hgrn_norm_w"][:, l, :]
            ps = self.ps
            dcount = [0]

            def dense_fm(col0, xt, xres, consume):
                bi = dcount[0] % 2
                dcount[0] += 1
                bank = ps[bi]
                for k in range(8):
                    S.op("pe", lambda e: e.matmul(bank[:, 0:T], Win[:, k, col0:col0 + 128], xt[:, k, :],
                                                  start=(k == 0), stop=(k == 7)),
                         r=[("Win", k), xres], w=["ps%d" % bi])
                consume(bank[:, 0:T], "ps%d" % bi)

            s_i = [0]
            own_blk = cfg["own_start"] * NBLK
            for t in range(cfg["ntiles"]):
                kv_only = t < cfg["kv_only"]
                pre_own = t < cfg["own_start"]
                xt = xTb[t % NXB]
                xres = ("xTb", t % NXB)
                S.dma("pool", xt[:], src[:, :, t * T:(t + 1) * T], w=[xres])
                if t == cfg["own_start"]:
                    S.op("dve", lambda e: e.tensor_scalar(out=St[:], in0=St[:], scalar1=self.flag[:, 0:1],
                                                          scalar2=None, op0=ALU.mult), r=["flag"], w=["St"])
                slot0 = (t * NBLK) % NSLOT
                kres = [("kc", slot0 + bb) for bb in range(NBLK)]
                for p in range(4):
                    if not kv_only:
                        def cqa(ap, rn, p=p):
                            S.op("act", lambda e: e.activation(out=qZ[0:64, 2 * p, :], in_=ap[0:64, :], func=AF.Copy,
                                                               scale=0.125), w=[rn, ("qZ", p)])
                            S.op("act", lambda e: e.activation(out=qZ[64:128, 2 * p + 1, :], in_=ap[64:128, :],
                                                               func=AF.Copy, scale=0.125), w=[rn, ("qZ", p)])
                        dense_fm(2048 + p * 128, xt, xres, cqa)
                    dense_fm(2560 + p * 128, xt, xres, lambda ap, rn, p=p: S.op(
                        "dve", lambda e: e.tensor_copy(out=Kc[:, p, slot0 * 128:slot0 * 128 + T], in_=ap),
                        w=[rn] + kres))
                for bb in range(NBLK):
                    slot = slot0 + bb
                    for k in range(8):
                        S.op("pe", lambda e: e.matmul(ps[2][:, :], xt[:, k, bb * 128:(bb + 1) * 128],
                                                      Win[:, k, 3072:3584], start=(k == 0), stop=(k == 7)),
                             r=[("Win", k), xres], w=["ps2"])
                    vps = ps[2][:, :].rearrange("p (a b) -> p a b", b=128)
                    if pre_own:
                        for (o0, i0) in ((0, 0), (128, 64)):
                            S.op("dve", lambda e: e.tensor_scalar(out=Vc[:, slot, :, o0:o0 + 64], in0=vps[:, :, i0:i0 + 64],
                                                                  scalar1=self.flag[:, 0:1], scalar2=None, op0=ALU.mult),
                                 r=["flag"], w=["ps2", ("vc", slot)])
                        S.op("dve", lambda e: e.tensor_copy(out=Vc[:, slot, :, 64:128],
                                                            in_=self.flag[:, 0:1].to_broadcast([128, 4, 64])),
                             r=["flag"], w=[("vc", slot)])
                    else:
                        S.op("act", lambda e: e.activation(out=Vc[:, slot, :, 0:64], in_=vps[:, :, 0:64], func=AF.Copy),
                             w=["ps2", ("vc", slot)])
                        S.op("dve", lambda e: e.tensor_copy(out=Vc[:, slot, :, 128:192], in_=vps[:, :, 64:128]),
                             w=["ps2", ("vc", slot)])
                        S.op("dve", lambda e: e.memset(Vc[:, slot, :, 64:128], 1.0), w=[("vc", slot)])
                for bb in range(NBLK):
                    for k in range(8):
                        S.op("pe", lambda e: e.matmul(ps[2][:, :], xt[:, k, bb * 128:(bb + 1) * 128],
                                                      Win[:, k, 1024:1536], start=(k == 0), stop=(k == 7)),
                             r=[("Win", k), xres], w=["ps2"])
                    S.op("act", lambda e: e.activation(out=vtok[:, bb, :], in_=ps[2][:, :], func=AF.Copy),
                         w=["ps2", "vtok"])
                A, Q, B, Kk, G, Dm, Ex = (hf[n] for n in ("A", "Q", "B", "Kk", "G", "Dm", "Ex"))
                def hg_front(h):
                    def cf(ap, rn):
                        S.op("act", lambda e: e.activation(out=B[:], in_=ap, func=AF.Exp, scale=-1.0),
                             w=[rn, "h_B"])
                        S.op("act", lambda e: e.activation(out=B[:], in_=B[:], func=AF.Ln, bias=1.0),
                             w=["h_B"])
                        S.op("act", lambda e: e.activation(out=B[:], in_=B[:], func=AF.Exp, scale=-1.0),
                             w=["h_B"])
                        S.op("dve", lambda e: e.tensor_scalar(out=B[:], in0=B[:], scalar1=omlv[:, h:h + 1],
                                                              scalar2=lbv[:, h:h + 1], op0=ALU.mult, op1=ALU.add),
                             r=["lb", "oml"], w=["h_B"])
                    dense_fm(512 + h * 128, xt, xres, cf)
                    if not kv_only:
                        def cq(ap, rn):
                            S.op("act", lambda e: e.activation(out=A[:], in_=ap, func=AF.Exp, scale=-1.0),
                                 w=[rn, "h_A"])
                            S.op("act", lambda e: e.activation(out=A[:], in_=A[:], func=AF.Ln, bias=1.0),
                                 w=["h_A"])
                            S.op("act", lambda e: e.activation(out=A[:], in_=A[:], func=AF.Exp, scale=-1.0),
                                 w=["h_A"])
                            S.op("dve", lambda e: e.tensor_tensor(out=Q[:], in0=ap, in1=A[:], op=ALU.mult),
                                 r=["h_A"], w=[rn, "h_Q"])
                        dense_fm(h * 128, xt, xres, cq)

                        def cg(ap, rn):
                            S.op("act", lambda e: e.activation(out=Sg2[h % 2][:], in_=ap, func=AF.Exp, scale=-1.0),
                                 w=[rn, ("h_Sg", h % 2)])
                            S.op("act", lambda e: e.activation(out=Sg2[h % 2][:], in_=Sg2[h % 2][:], func=AF.Ln, bias=1.0),
                                 w=[("h_Sg", h % 2)])
                            S.op("act", lambda e: e.activation(out=Sg2[h % 2][:], in_=Sg2[h % 2][:], func=AF.Exp, scale=-1.0),
                                 w=[("h_Sg", h % 2)])
                        dense_fm(1536 + h * 128, xt, xres, cg)

                    S.op("dve", lambda e: e.tensor_scalar(out=Kk[:], in0=B[:], scalar1=-1.0, scalar2=1.0,
                                                          op0=ALU.mult, op1=ALU.add), r=["h_B"], w=["h_Kk"])
                    S.op("act", lambda e: e.activation(out=B[:], in_=B[:], func=AF.Ln), w=["h_B"])
                    S.op("dve", lambda e: e.tensor_tensor_scan(out=G[:], data0=self.reset[:], data1=B[:],
                                                               initial=0.0, op0=ALU.mult, op1=ALU.add),
                         r=["reset", "h_B"], w=["h_G"])
                    Gv = G[:].rearrange("p (c t) -> p c t", t=64)
                    Dv = Dm[:].rearrange("p (c t) -> p c t", t=64)
                    S.op("dve", lambda e: e.tensor_tensor(out=Dv, in0=Gv,
                                                          in1=Gv[:, :, REF:REF + 1].to_broadcast([128, NCH, 64]),
                                                          op=ALU.subtract), r=["h_G"], w=["h_Dm"])
                    S.op("act", lambda e: e.activation(out=Ex[:], in_=Dm[:], func=AF.Exp, scale=-1.0),
                         r=["h_Dm"], w=["h_Ex"])
                    S.op("dve", lambda e: e.tensor_tensor(out=kt[:], in0=Kk[:], in1=Ex[:], op=ALU.mult),
                         r=["h_Kk", "h_Ex"], w=["kt"])
                    if not kv_only:
                        S.op("act", lambda e: e.activation(out=Ex[:], in_=Dm[:], func=AF.Exp),
                             r=["h_Dm"], w=["h_Ex"])
                        S.op("dve", lambda e: e.tensor_tensor(out=qt2[h % 2][:], in0=Q[:], in1=Ex[:], op=ALU.mult),
                             r=["h_Q", "h_Ex"], w=[("qt", h % 2)])
                    S.op("dve", lambda e: e.tensor_tensor(out=sc["dl"][:], in0=Gv[:, :, 63], in1=Gv[:, :, REF],
                                                          op=ALU.subtract), r=["h_G"], w=["sc_dl"])
                    S.op("act", lambda e: e.activation(out=sc["ea"][:], in_=Gv[:, :, 63], func=AF.Exp),
                         r=["h_G"], w=["sc_ea"])
                    S.op("act", lambda e: e.activation(out=sc["eb"][:], in_=sc["dl"][:], func=AF.Exp),
                         r=["sc_dl"], w=["sc_eb"])
                    S.op("act", lambda e: e.activation(out=sc["er"][:], in_=Gv[:, :, REF], func=AF.Exp),
                         r=["h_G"], w=["sc_er"])
                def hg_back1(h):
                    for bb in range(NBLK):
                        S.op("pe", lambda e: e.matmul(ps[2][:, bb * 128:(bb + 1) * 128],
                                                      kt[:, bb * 128:(bb + 1) * 128], self.ident[:],
                                                      start=True, stop=True),
                             r=["kt", "ident"], w=["ps2"])
                    S.op("act", lambda e: e.activation(out=ktokE[0:64, :, :].rearrange("p c k -> p (c k)"),
                                                       in_=ps[2][0:64, :], func=AF.Copy), w=["ps2", "ktok"])
                    S.op("act", lambda e: e.activation(out=ktokO[64:128, :, :].rearrange("p c k -> p (c k)"),
                                                       in_=ps[2][64:128, :], func=AF.Copy), w=["ps2", "ktok"])
                    if not kv_only:
                        for bb in range(NBLK):
                            S.op("pe", lambda e: e.matmul(ps[7][:, bb * 128:(bb + 1) * 128],
                                                          kt[:, bb * 128:(bb + 1) * 128],
                                                          qt2[h % 2][:, bb * 128:(bb + 1) * 128],
                                                          start=True, stop=True),
                                 r=["kt", ("qt", h % 2)], w=["ps7"])
                        S.op("dve", lambda e: e.tensor_tensor(
                            out=ATs[:].rearrange("p (c t) -> p c t", t=128),
                            in0=ps[7][:, 0:T].rearrange("p (c t) -> p c t", t=128),
                            in1=self.tri[:].rearrange("p (o t) -> p o t", o=1).to_broadcast([128, NBLK, 128]),
                            op=ALU.mult), r=["tri"], w=["ps7", "ATs"])
                    for c in range(NCH):
                        ub = 3 + c // 4
                        kz = ktokE if c % 2 == 0 else ktokO
                        S.op("pe", lambda e: e.matmul(ps[ub][:, (c % 4) * 128:(c % 4 + 1) * 128], kz[:, c // 2, :],
                                                      vtok[:, c // 2, h * 128:(h + 1) * 128], start=True, stop=True),
                             r=["ktok", "vtok"], w=["ps%d" % ub])
                    for c in range(NCH):
                        ub = 3 + c // 4
                        if not kv_only:
                            S.op("dve", lambda e: e.tensor_scalar(out=Sp[:, c, :], in0=St[:, h, :],
                                                                  scalar1=sc["er"][:, c:c + 1], scalar2=None,
                                                                  op0=ALU.mult), r=["St", "sc_er"], w=["Sp"])
                        S.op("dve", lambda e: e.tensor_scalar(out=tmpU[:], in0=ps[ub][:, (c % 4) * 128:(c % 4 + 1) * 128],
                                                              scalar1=sc["eb"][:, c:c + 1], scalar2=None,
                                                              op0=ALU.mult), r=["sc_eb"], w=["ps%d" % ub, "tmpU"])
                        S.op("dve", lambda e: e.scalar_tensor_tensor(out=St[:, h, :], in0=St[:, h, :],
                                                                     scalar=sc["ea"][:, c:c + 1], in1=tmpU[:],
                                                                     op0=ALU.mult, op1=ALU.add),
                             r=["sc_ea", "tmpU"], w=["St"])
                def hg_back2(h):
                    if not kv_only:
                        for bb in range(NBLK):
                            S.op("pe", lambda e: e.matmul(ps[5][:, bb * 128:(bb + 1) * 128],
                                                          vtok[:, bb, h * 128:(h + 1) * 128],
                                                          ATs[:, bb * 128:(bb + 1) * 128], start=True, stop=False),
                                 r=["vtok", "ATs"], w=["ps5"])
                            for c in (2 * bb, 2 * bb + 1):
                                S.op("pe", lambda e: e.matmul(ps[5][:, c * 64:(c + 1) * 64], Sp[:, c, :],
                                                              qt2[h % 2][:, c * 64:(c + 1) * 64], start=False,
                                                              stop=(c == 2 * bb + 1)),
                                     r=["Sp", ("qt", h % 2)], w=["ps5"])
                        S.op("act", lambda e: e.activation(out=A[:], in_=ps[5][:, :], func=AF.Square),
                             w=["ps5", "h_A"])
                        S.op("pe", lambda e: e.matmul(ps[6][:, :], self.avg128[:], A[:], start=True, stop=True),
                             r=["h_A", "avg128"], w=["ps6"])
                        S.op("dve", lambda e: e.tensor_scalar(out=T1[:], in0=ps[6][:, :], scalar1=RMS_EPS,
                                                              scalar2=None, op0=ALU.add), w=["ps6", "h_T1"])
                        S.op("act", lambda e: e.activation(out=T1[:], in_=T1[:], func=AF.Ln), w=["h_T1"])
                        S.op("act", lambda e: e.activation(out=T1[:], in_=T1[:], func=AF.Exp, scale=-0.5), w=["h_T1"])
                        S.op("dve", lambda e: e.tensor_tensor(out=T1[:], in0=ps[5][:, :], in1=T1[:], op=ALU.mult),
                             w=["ps5", "h_T1"])
                        S.op("dve", lambda e: e.scalar_tensor_tensor(out=oT[:, h, :], in0=T1[:],
                                                                     scalar=nw[:, h:h + 1], in1=Sg2[h % 2][:],
                                                                     op0=ALU.mult, op1=ALU.mult),
                             r=["h_T1", ("h_Sg", h % 2), "v_hgrn_norm_w"], w=[("oT", h)])
                hg_front(0)
                if not kv_only:
                    qb0 = t * NBLK
                    items = []
                    for hh in range(8):
                        ds = [d for d in range(NBLK + 16) if qb0 + NBLK - 1 - d >= 0]
                        ds = [d for d in ds if NBLK - 1 <= d <= 16] + [d for d in ds if d < NBLK - 1 or d > 16]
                        for i, d in enumerate(ds):
                            items.append((hh, i, d, i == len(ds) - 1))
                    LAG = 2
                    STEP = 2
                    sbanks = (3, 4, 0, 1)
                    bufs = {}

                    def cols(d):
                        qlo = max(0, NBLK - 1 - d)
                        qhi = min(NBLK - 1, NBLK - 1 + 16 - d)
                        return qlo * 128, (qhi + 1) * 128

                    def emit_qk(n):
                        hh, i, d, last = items[n]
                        c0, c1 = cols(d)
                        p, half = hh // 2, hh % 2
                        pr = slice(half * 64, half * 64 + 64)
                        ks = (qb0 + NBLK - 1 - d) % NSLOT
                        bi = s_i[0] % 4
                        s_i[0] += 1
                        bufs[n] = bi
                        sbk = sbanks[bi]
                        S.op("pe", lambda e: e.matmul(ps[sbk][:, c0:c1], Kc[:, p, ks * 128:(ks + 1) * 128],
                                                      qZ[:, hh, c0:c1], start=True, stop=True),
                             r=[("kc", ks), ("qZ", p)], w=["ps%d" % sbk])
                        S.op("act", lambda e: e.activation(out=Eb[bi][:, c0:c1], in_=ps[sbk][:, c0:c1], func=AF.Exp),
                             w=["ps%d" % sbk, ("Eb", bi)])
                        mk = self.amask[:, d:d + NBLK, :].rearrange("p a b -> p (a b)")
                        S.op("dve", lambda e: e.tensor_tensor(out=Eb[bi][:, c0:c1], in0=Eb[bi][:, c0:c1],
                                                              in1=mk[:, c0:c1], op=ALU.mult),
                             r=["amask"], w=[("Eb", bi)])

                    dly = min(8, len(items) // 8)

                    def emit_pv(n):
                        hh, i, d, last = items[n]
                        c0, c1 = cols(d)
                        p, half = hh // 2, hh % 2
                        pr = slice(half * 64, half * 64 + 64)
                        ks = (qb0 + NBLK - 1 - d) % NSLOT
                        bi = bufs.pop(n)
                        bn = (5, 6, 7, 2)[hh % 4]
                        lw = Vc[:, ks, p, 0:128] if half == 0 else Vc[:, ks, p, 64:192]
                        S.op("pe", lambda e: e.matmul(ps[bn][:, c0:c1], lw, Eb[bi][:, c0:c1],
                                                      start=(i == 0), stop=last),
                             r=[("vc", ks), ("Eb", bi)], w=["ps%d" % bn])
                        if last:
                            dpr = slice(64, 128) if half == 0 else slice(0, 64)
                            rd, rd2 = rden[0], rden[1]

                            def fin(bn=bn, pr=pr, p=p, half=half):
                                S.op("dve", lambda e: e.tensor_tensor(out=oT[pr, 4 + p, :], in0=ps[bn][pr, :],
                                                                      in1=rd2[pr, :], op=ALU.mult),
                                     r=[("rd2", half)], w=["ps%d" % bn, ("oT", 4 + p)])

                            def chain(bn=bn, pr=pr, dpr=dpr, half=half, n=n, fin=fin):
                                for it in [x for x in deferred if x[2] == "fin"]:
                                    deferred.remove(it)
                                    it[1]()
                                S.op("act", lambda e: e.activation(out=rd[dpr, :], in_=ps[bn][dpr, :], func=AF.Ln,
                                                                   bias=1e-30),
                                     w=["ps%d" % bn, ("rd", half)])
                                S.op("act", lambda e: e.activation(out=rd[dpr, :], in_=rd[dpr, :], func=AF.Exp,
                                                                   scale=-1.0), w=[("rd", half)])
                                S.dma("sp", rd2[pr, :], rd[dpr, :], r=[("rd", half)], w=[("rd2", half)])
                                deferred.append((n + 2 * dly, fin, "fin"))
                            deferred.append((n + dly, chain, "chain"))

                    deferred = []
                    nsteps = (len(items) + STEP - 1) // STEP
                    for st in range(nsteps + 1):
                        for n in range(st * STEP, min((st + 1) * STEP, len(items))):
                            emit_qk(n)
                        if st >= 1:
                            for n in range((st - 1) * STEP, min(st * STEP, len(items))):
                                emit_pv(n)
                        for it in [x for x in deferred if x[0] <= st * STEP]:
                            if it in deferred:
                                deferred.remove(it)
                                it[1]()
                    while deferred:
                        deferred.pop(0)[1]()
                for h in range(4):
                    hg_back1(h)
                    if h < 3:
                        hg_front(h + 1)
                    hg_back2(h)
                if kv_only:
                    continue
                tf = t - cfg["kv_only"]
                for m in range(8):
                    bi = dcount[0] % 2
                    dcount[0] += 1
                    bank = ps[bi]
                    for j in range(8):
                        S.op("pe", lambda e: e.matmul(bank[:, 0:T], Wo[:, j, m * 128:(m + 1) * 128], oT[:, j, :],
                                                      start=(j == 0), stop=(j == 7)),
                             r=["Wo", ("oT", j)], w=["ps%d" % bi])
                    xrb, zb = xr[m % 2], zr[m % 2]
                    S.dma("sp", xrb[:], src[:, m, t * T:(t + 1) * T], w=[("xr", m % 2)])
                    S.op("dve", lambda e: e.scalar_tensor_tensor(out=zb[:], in0=xrb[:], scalar=ALPHA,
                                                                 in1=bank[:, 0:T], op0=ALU.mult, op1=ALU.add),
                         r=[("xr", m % 2)], w=["ps%d" % bi, ("zr", m % 2)])
                    S.dma("sp", x1a[:, m, tf * T:(tf + 1) * T], zb[:], r=[("zr", m % 2)], w=[("x1a", tf)])

    def phase_b(self, cfg, x1a, dst, nfull):
        nc, S = self.nc, self.S
        l = cfg["l"]
        ps = self.ps
        with ExitStack() as es:
            sb = lambda n, s, d: self.sb(n + "_b%d" % l, s, d, es)
            W1 = sb("W1", [128, 8, 4096], BF16)
            W2 = sb("W2", [128, 32, D], BF16)
            for k in range(8):
                S.dma("pool", W1[:, k, :], self.w_ff1[l, k * 128:(k + 1) * 128, :], w=[("W1", k)])
            for k in range(32):
                S.dma("pool", W2[:, k, :], self.w_ff2[l, k * 128:(k + 1) * 128, :], w=[("W2", k)])
            xfb = [sb("x1f%d" % i, [128, 8, T], F32) for i in range(2)]
            x1b = sb("x1b", [128, 8, T], BF16)
            rr = [sb("rr%d" % i, [128, T], BF16) for i in range(2)]
            self.ln_zs = [sb("lnzs%d" % i, [128, T], BF16) for i in range(2)]
            self.ln_zc = [sb("lnzc%d" % i, [128, T], BF16) for i in range(2)]
            NH = 16
            hT = sb("hT", [128, NH, T], BF16)
            self.lnring = [sb("lnrb%d" % i, [128, T], F32) for i in range(2)]
            self.ln_mean = sb("ln_meanb", [128, T], F32)
            self.ln_rstd = sb("ln_rstdb", [128, T], F32)
            outr = self.lnring
            g1, b1 = self.vsb["ln1_g"][:, l, :], self.vsb["ln1_b"][:, l, :]
            g2, b2 = self.vsb["ln2_g"][:, l, :], self.vsb["ln2_b"][:, l, :]

            def load(tf):
                xf = xfb[tf % 2]
                for m in range(8):
                    S.dma("sp", xf[:, m, :], x1a[:, m, tf * T:(tf + 1) * T], r=[("x1a", tf)], w=[("x1f", tf % 2, m)])

            def ln1(tf):
                xf = xfb[tf % 2]
                zres = lambda m: ("x1f", tf % 2, m)

                def emit1(m):
                    S.op("act", lambda e: e.activation(out=x1b[:, m, :], in_=xf[:, m, :], func=AF.Identity,
                                                       scale=g1[:, m:m + 1], bias=b1[:, m:m + 1]),
                         r=["v_ln1_g", "v_ln1_b", zres(m)], w=[("x1b", m)])
                    S.op("act", lambda e: e.activation(out=xf[:, m, :], in_=xf[:, m, :], func=AF.Identity,
                                                       scale=g1[:, m:m + 1], bias=b1[:, m:m + 1]),
                         r=["v_ln1_g", "v_ln1_b"], w=[zres(m)])
                self.layer_norm(xf, zres, emit1)

            load(0)
            ln1(0)
            for tf in range(nfull):
                xf = xfb[tf % 2]
                zres = lambda m: ("x1f", tf % 2, m)
                if tf + 1 < nfull:
                    load(tf + 1)
                for half in range(2):
                    for jj in range(NH):
                        j = half * NH + jj
                        bi = 2 + (j % 2)
                        for k in range(8):
                            S.op("pe", lambda e: e.matmul(ps[bi][:, 0:T], W1[:, k, j * 128:(j + 1) * 128],
                                                          x1b[:, k, :], start=(k == 0), stop=(k == 7)),
                                 r=[("W1", k), ("x1b", k)], w=["ps%d" % bi])
                        rb = rr[j % 2]
                        S.op("act", lambda e: e.activation(out=rb[:], in_=ps[bi][:, 0:T], func=AF.Relu),
                             w=["ps%d" % bi, ("rr", j % 2)])
                        S.op("dve", lambda e: e.scalar_tensor_tensor(out=hT[:, jj, :], in0=ps[bi][:, 0:T], scalar=0.0,
                                                                     in1=rb[:], op0=ALU.max, op1=ALU.mult),
                             r=[("rr", j % 2)], w=["ps%d" % bi, ("hT", jj)])
                    if half == 1 and tf + 1 < nfull:
                        ln1(tf + 1)
                    for m in range(8):
                        bi = 4 + (m % 2)
                        for jj in range(NH):
                            j = half * NH + jj
                            S.op("pe", lambda e: e.matmul(ps[bi][:, 0:T], W2[:, j, m * 128:(m + 1) * 128],
                                                          hT[:, jj, :], start=(jj == 0), stop=(jj == NH - 1)),
                                 r=[("W2", j), ("hT", jj)], w=["ps%d" % bi])
                        if half == 0:
                            S.op("dve", lambda e: e.scalar_tensor_tensor(out=xf[:, m, :], in0=xf[:, m, :],
                                                                         scalar=ALPHA, in1=ps[bi][:, 0:T],
                                                                         op0=ALU.mult, op1=ALU.add),
                                 w=["ps%d" % bi, zres(m)])
                        else:
                            S.op("dve", lambda e: e.tensor_tensor(out=xf[:, m, :], in0=xf[:, m, :],
                                                                  in1=ps[bi][:, 0:T], op=ALU.add),
                                 w=["ps%d" % bi, zres(m)])

                def emit2(m):
                    ob = outr[m % 2]
                    S.op("act", lambda e: e.activation(out=ob[:], in_=xf[:, m, :], func=AF.Identity,
                                                       scale=g2[:, m:m + 1], bias=b2[:, m:m + 1]),
                         r=[zres(m), "v_ln2_g", "v_ln2_b"], w=[("lnr", m % 2)])
                    S.dma("sp", dst[:, m, tf * T:(tf + 1) * T], ob[:], r=[("lnr", m % 2)],
                          w=[("dst", id(dst), tf, m)])
                self.layer_norm(xf, zres, emit2)


VEC_NAMES = ("lower_bounds", "hgrn_norm_w", "ln1_g", "ln1_b", "ln2_g", "ln2_b")


def _common_inputs(inp):
    m = dict(host_consts())
    for nm in ("w_in", "w_out", "w_ff1", "w_ff2"):
        m[nm] = np.ascontiguousarray(inp[nm], dtype=np.float32)
    for nm in VEC_NAMES:
        v = np.asarray(inp[nm], dtype=np.float32)
        m[nm] = np.ascontiguousarray(v.reshape(2, v.shape[1] // 128, 128).transpose(0, 2, 1))
    return m


_NC_CACHE = {}


def _fused_nc():
    if "fused" not in _NC_CACHE:
        cfgs = [dict(l=0, ntiles=16, kv_only=4, own_start=8), dict(l=1, ntiles=12, kv_only=4, own_start=4)]
        _NC_CACHE["fused"] = Builder(8192, cfgs, 4096).build()
    return _NC_CACHE["fused"]


def kernel(x, w_in, w_out, lower_bounds, hgrn_norm_w, ln1_g, ln1_b, w_ff1, w_ff2, ln2_g, ln2_b):
    inp = dict(w_in=w_in, w_out=w_out, lower_bounds=lower_bounds, hgrn_norm_w=hgrn_norm_w, ln1_g=ln1_g,
               ln1_b=ln1_b, w_ff1=w_ff1, w_ff2=w_ff2, ln2_g=ln2_g, ln2_b=ln2_b)
    common = _common_inputs(inp)
    x = np.asarray(x, dtype=np.float32)
    B, Sq, _ = x.shape
    OWN, HALO = 4096, 4096
    in_maps = []
    for c in range(8):
        b, q = c // 4, c % 4
        st = q * OWN
        xs = np.zeros((HALO + OWN, D), np.float32)
        lo = st - HALO
        if lo < 0:
            xs[-lo:] = x[b, 0:st + OWN]
        else:
            xs[:] = x[b, lo:st + OWN]
        m = dict(common)
        m["xT"] = np.ascontiguousarray(xs.T)
        m["flag"] = np.full((128, 1), 0.0 if q == 0 else 1.0, np.float32)
        in_maps.append(m)
    nc = _fused_nc()
    res = run_bass_kernel_spmd(nc, in_maps, core_ids=list(range(8)))
    out = np.empty((B, Sq, D), np.float32)
    for c in range(8):
        b, q = c // 4, c % 4
        out[b, q * OWN:(q + 1) * OWN] = res.results[c]["outT"].T
    return out
```

```python
import numpy as np
import ml_dtypes
from contextlib import ExitStack
import concourse.bass as bass
import concourse.mybir as mybir
from concourse.bass_utils import run_bass_kernel_spmd

F32 = mybir.dt.float32
BF16 = mybir.dt.bfloat16
AF = mybir.ActivationFunctionType
ALU = mybir.AluOpType

D = 1024
T = 512
NBLK = T // 128
NCH = T // 64
NSLOT = 20
NDELTA = 17
NMASK = NDELTA + 2 * (NBLK - 1)
ALPHA = 4.0 ** 0.25
LN_EPS = 1e-5
RMS_EPS = 1e-6
REF = 31


class Sync:
    EPOCH = 30000
    NDMA = 24
    NHW = 16

    def __init__(self, nc, es):
        self.nc, self.es = nc, es
        self.eng = {"pe": nc.tensor, "act": nc.scalar, "dve": nc.vector, "pool": nc.gpsimd, "sp": nc.sync}
        self.cnt = {e: 0 for e in self.eng}
        self.sems = {e: [] for e in self.eng}
        self.waited = {e: {} for e in self.eng}
        self.lastw = {}
        self.readers = {}
        self.dsem = [es.enter_context(nc.semaphore("dma%d" % i)) for i in range(self.NDMA)]
        self.dcnt = [0] * self.NDMA
        self.dnext = 0
        self.dnext_sw = 0
        self.nwaits = 0

    def _esem(self, e, ep):
        while len(self.sems[e]) <= ep:
            self.sems[e].append(self.es.enter_context(self.nc.semaphore("c_%s_%d" % (e, len(self.sems[e])))))
        return self.sems[e][ep]

    def _wait(self, eng, key, n):
        if n <= 0:
            return
        if self.waited[eng].get(key, 0) >= n:
            return
        self.waited[eng][key] = n
        self.nwaits += 1
        if key[0] == "e":
            ep = (n - 1) // self.EPOCH
            self.eng[eng].wait_ge(self._esem(key[1], ep), n - ep * self.EPOCH)
        else:
            self.eng[eng].wait_ge(self.dsem[key[1]], n)

    def _deps(self, eng, r, w):
        deps = {}

        def add(kn):
            if kn is None:
                return
            k, n = kn
            if k == ("e", "pe") and eng == "pe":
                return
            if deps.get(k, 0) < n:
                deps[k] = n
        for x in r:
            add(self.lastw.get(x))
        for x in w:
            add(self.lastw.get(x))
            for kn in self.readers.get(x, {}).items():
                add(kn)
        for k, n in deps.items():
            self._wait(eng, k, n)

    def _record(self, key, n, r, w):
        for x in r:
            self.readers.setdefault(x, {})[key] = n
        for x in w:
            self.lastw[x] = (key, n)
            self.readers[x] = {}

    def op(self, eng, fn, r=(), w=()):
        self._deps(eng, r, w)
        ins = fn(self.eng[eng])
        self.cnt[eng] += 1
        n = self.cnt[eng]
        ep = (n - 1) // self.EPOCH
        ins.then_inc(self._esem(eng, ep), 1)
        self._record(("e", eng), n, r, w)
        return ins

    def dma(self, q, out, in_, r=(), w=()):
        if q == "pool":
            s = self.NHW + self.dnext_sw
            self.dnext_sw = (self.dnext_sw + 1) % (self.NDMA - self.NHW)
        else:
            s = self.dnext
            self.dnext = (self.dnext + 1) % self.NHW
        self._wait(q, ("d", s), self.dcnt[s])
        self._deps(q, r, w)
        ins = self.eng[q].dma_start(out=out, in_=in_)
        self.dcnt[s] += 16
        ins.then_inc(self.dsem[s], 16)
        self._record(("d", s), self.dcnt[s], r, w)

    def barrier(self):
        for e in self.eng:
            for x in self.eng:
                if x != e:
                    self._wait(e, ("e", x), self.cnt[x])
            for s in range(self.NDMA):
                self._wait(e, ("d", s), self.dcnt[s])

    def finish(self, eng="sp"):
        for x in self.eng:
            if x != eng:
                self._wait(eng, ("e", x), self.cnt[x])
        for s in range(self.NDMA):
            self._wait(eng, ("d", s), self.dcnt[s])


def host_consts():
    ident = np.eye(128, dtype=np.float32).astype(ml_dtypes.bfloat16)
    s = np.arange(64)
    tri64 = (s[:, None] <= s[None, :]).astype(np.float32)
    tri = np.zeros((128, 128), np.float32)
    tri[0:64, 0:64] = tri64
    tri[64:128, 64:128] = tri64
    tri = tri.astype(ml_dtypes.bfloat16)
    j = np.arange(128)[:, None]
    i = np.arange(128)[None, :]
    am = np.zeros((128, NMASK, 128), np.float32)
    for a in range(NMASK):
        dist = 128 * (a - (NBLK - 1)) + i - j
        for (win, dil) in ((128, 1), (512, 4), (2048, 16)):
            am[:, a, :] += ((dist >= 0) & (dist <= win) & (dist % dil == 0)).astype(np.float32)
    am = am.astype(ml_dtypes.bfloat16)
    reset = np.ones((128, T), np.float32)
    reset[:, ::64] = 0.0
    return {"c_ident": ident, "c_tri": tri, "c_amask": am, "c_reset": reset}


class Builder:
    def __init__(self, ntok_in, layer_cfgs, ntok_out, debug=None):
        self.nc = nc = bass.Bass("TRN2", target_bir_lowering=False)
        self.es = ExitStack()
        self.ntok_in, self.ntok_out = ntok_in, ntok_out
        dt = nc.dram_tensor
        self.xT = dt("xT", [D, ntok_in], F32, kind="ExternalInput").ap()
        self.w_in = dt("w_in", [2, D, 3584], F32, kind="ExternalInput").ap()
        self.w_out = dt("w_out", [2, D, D], F32, kind="ExternalInput").ap()
        self.w_ff1 = dt("w_ff1", [2, D, 4096], F32, kind="ExternalInput").ap()
        self.w_ff2 = dt("w_ff2", [2, 4096, D], F32, kind="ExternalInput").ap()
        self.vecs = {}
        for nm, n in (("lower_bounds", 512), ("hgrn_norm_w", 512), ("ln1_g", D), ("ln1_b", D),
                      ("ln2_g", D), ("ln2_b", D)):
            self.vecs[nm] = dt(nm, [2, 128, n // 128], F32, kind="ExternalInput").ap()
        self.flag_d = dt("flag", [128, 1], F32, kind="ExternalInput").ap()
        self.c_ident = dt("c_ident", [128, 128], BF16, kind="ExternalInput").ap()
        self.c_tri = dt("c_tri", [128, 128], BF16, kind="ExternalInput").ap()
        self.c_amask = dt("c_amask", [128, NMASK, 128], BF16, kind="ExternalInput").ap()
        self.c_reset = dt("c_reset", [128, T], F32, kind="ExternalInput").ap()
        self.outT = dt("outT", [D, ntok_out], F32, kind="ExternalOutput").ap()
        self.layer_cfgs = layer_cfgs
        self.debug = debug

    def sb(self, name, shape, dtype, es=None):
        return (es or self.es).enter_context(self.nc.sbuf_tensor(name, shape, dtype))

    def build(self):
        nc, es = self.nc, self.es
        with es:
            self.S = S = Sync(nc, es)
            self.setup_consts()
            self.ps = [es.enter_context(nc.psum_tensor("ps%d" % i, [128, 512], F32)) for i in range(8)]
            nlay = len(self.layer_cfgs)
            src = self.xT.rearrange("(c p) t -> p c t", p=128)
            for li, cfg in enumerate(self.layer_cfgs):
                nfull = cfg["ntiles"] - cfg["kv_only"]
                x1a = nc.dram_tensor("x1a_%d" % li, [128, 8, nfull * T], F32, kind="Internal").ap()
                if li == nlay - 1:
                    dst = self.outT.rearrange("(c p) t -> p c t", p=128)
                else:
                    dst = nc.dram_tensor("xmid_%d" % li, [128, 8, nfull * T], F32, kind="Internal").ap()
                self.phase_a(cfg, src, x1a)
                S.barrier()
                self.phase_b(cfg, x1a, dst, nfull)
                S.barrier()
                src = dst
            S.finish("sp")
        return nc

    def setup_consts(self):
        nc, S = self.nc, None
        S = self.S
        self.ident = self.sb("ident", [128, 128], BF16)
        self.tri = self.sb("tri", [128, 128], BF16)
        self.amask = self.sb("amask", [128, NMASK, 128], BF16)
        self.reset = self.sb("reset", [128, T], F32)
        self.flag = self.sb("flag_sb", [128, 1], F32)
        self.avgb = self.sb("avgb", [128, 128], BF16)
        self.avg128 = self.sb("avg128", [128, 128], F32)
        S.dma("sp", self.ident[:], self.c_ident, w=["ident"])
        S.dma("sp", self.tri[:], self.c_tri, w=["tri"])
        S.dma("sp", self.amask[:], self.c_amask, w=["amask"])
        S.dma("sp", self.reset[:], self.c_reset, w=["reset"])
        S.dma("sp", self.flag[:], self.flag_d, w=["flag"])
        S.op("dve", lambda e: e.memset(self.avgb[:], 1.0 / 1024), w=["avgb"])
        S.op("dve", lambda e: e.memset(self.avg128[:], 1.0 / 128), w=["avg128"])
        self.vsb = {}
        for nm, ap in self.vecs.items():
            n = ap.shape[2]
            t = self.sb("v_" + nm, [128, 2, n], F32)
            for l in range(2):
                S.dma("sp", t[:, l, :], ap[l], w=["v_" + nm])
            self.vsb[nm] = t
        lbt = self.vsb["lower_bounds"]
        e = self.sb("lb_e", [128, 2, 4], F32)
        self.lb = self.sb("lb", [128, 2, 4], F32)
        self.oml = self.sb("oml", [128, 2, 4], F32)
        ssum = self.sb("lb_s", [128, 4], F32)
        S.op("act", lambda en: en.activation(out=e[:], in_=lbt[:], func=AF.Exp), r=["v_lower_bounds"], w=["lb_e"])
        S.op("dve", lambda en: en.tensor_tensor(out=ssum[:], in0=e[:, 0, :], in1=e[:, 1, :], op=ALU.add),
             r=["lb_e"], w=["lb_s"])
        S.op("dve", lambda en: en.reciprocal(out=ssum[:], in_=ssum[:]), r=["lb_s"], w=["lb_s"])
        S.op("dve", lambda en: en.tensor_tensor(out=self.lb[:, 0, :], in0=e[:, 0, :], in1=e[:, 0, :], op=ALU.subtract),
             r=["lb_e", "lb_s"], w=["lb"])
        S.op("dve", lambda en: en.tensor_tensor(out=self.lb[:, 1, :], in0=e[:, 1, :], in1=ssum[:], op=ALU.mult),
             r=["lb_e", "lb_s", "lb"], w=["lb"])
        S.op("dve", lambda en: en.tensor_scalar(out=self.oml[:], in0=self.lb[:], scalar1=-1.0, scalar2=1.0,
                                                op0=ALU.mult, op1=ALU.add), r=["lb"], w=["oml"])

    def layer_norm(self, zbuf, zres, emit_chunk, part="all"):
        S = self.S
        psm, psq = self.ps[0], self.ps[1]
        mean, rstd = self.ln_mean, self.ln_rstd
        if part in ("all", "stats"):
            self._ln_stats(zbuf, zres, psm, psq, mean, rstd)
        if part in ("all", "apply"):
            for m in range(8):
                S.op("dve", lambda e: e.tensor_tensor(out=zbuf[:, m, :], in0=zbuf[:, m, :], in1=mean[:],
                                                      op=ALU.subtract), r=["ln_mean"], w=[zres(m)])
                S.op("dve", lambda e: e.tensor_tensor(out=zbuf[:, m, :], in0=zbuf[:, m, :], in1=rstd[:],
                                                      op=ALU.mult), r=["ln_rstd"], w=[zres(m)])
                emit_chunk(m)

    def _ln_stats(self, zbuf, zres, psm, psq, mean, rstd):
        S = self.S
        for m in range(8):
            zs, zc = self.ln_zs[m % 2], self.ln_zc[m % 2]
            S.op("act", lambda e: e.activation(out=zs[:], in_=zbuf[:, m, :], func=AF.Square),
                 r=[zres(m)], w=[("lnzs", m % 2)])
            S.op("dve", lambda e: e.tensor_copy(out=zc[:], in_=zbuf[:, m, :]), r=[zres(m)], w=[("lnzc", m % 2)])
            S.op("pe", lambda e: e.matmul(psm[:, 0:T], self.avgb[:], zc[:], start=(m == 0), stop=(m == 7)),
                 r=[("lnzc", m % 2), "avgb"], w=["ps0"])
            S.op("pe", lambda e: e.matmul(psq[:, 0:T], self.avgb[:], zs[:], start=(m == 0), stop=(m == 7)),
                 r=[("lnzs", m % 2), "avgb"], w=["ps1"])
        S.op("act", lambda e: e.activation(out=mean[:], in_=psm[:, 0:T], func=AF.Copy), w=["ps0", "ln_mean"])
        S.op("act", lambda e: e.activation(out=rstd[:], in_=psm[:, 0:T], func=AF.Square), w=["ps0", "ln_rstd"])
        S.op("dve", lambda e: e.tensor_tensor(out=rstd[:], in0=psq[:, 0:T], in1=rstd[:], op=ALU.subtract),
             w=["ps1", "ln_rstd"])
        S.op("dve", lambda e: e.tensor_scalar(out=rstd[:], in0=rstd[:], scalar1=LN_EPS, scalar2=None, op0=ALU.add),
             w=["ln_rstd"])
        S.op("act", lambda e: e.activation(out=rstd[:], in_=rstd[:], func=AF.Ln), w=["ln_rstd"])
        S.op("act", lambda e: e.activation(out=rstd[:], in_=rstd[:], func=AF.Exp, scale=-0.5), w=["ln_rstd"])

    def phase_a(self, cfg, src, x1a):
        nc, S = self.nc, self.S
        l = cfg["l"]
        with ExitStack() as es:
            sb = lambda n, s, d: self.sb(n + "_a%d" % l, s, d, es)
            Win = sb("Win", [128, 8, 3584], BF16)
            Wo = sb("Wo", [128, 8, D], BF16)
            for k in range(8):
                S.dma("pool", Win[:, k, :], self.w_in[l, k * 128:(k + 1) * 128, :], w=[("Win", k)])
            for k in range(8):
                S.dma("pool", Wo[:, k, :], self.w_out[l, k * 128:(k + 1) * 128, :], w=["Wo"])
            NXB = 1
            xTb = [sb("xTb%d" % i, [128, 8, T], BF16) for i in range(NXB)]
            xr = [sb("xr%d" % i, [128, T], F32) for i in range(2)]
            zr = [sb("zr%d" % i, [128, T], F32) for i in range(2)]
            hf = {n: sb("h_" + n, [128, T], F32) for n in ("A", "Q", "B", "Kk", "G", "Dm", "Ex")}
            tmpU = sb("tmpU", [128, 128], F32)
            qt2 = [sb("qt%d" % i, [128, T], BF16) for i in range(2)]
            Sg2 = [sb("Sg%d" % i, [128, T], F32) for i in range(2)]
            T1 = sb("T1", [128, T], F32)
            kt = sb("kt", [128, T], BF16)
            ktokE = sb("ktokE", [128, NBLK, 128], BF16)
            ktokO = sb("ktokO", [128, NBLK, 128], BF16)
            ATs = sb("ATs", [128, T], BF16)
            vtok = sb("vtok", [128, NBLK, 512], BF16)
            Sp = sb("Sp", [128, NCH, 128], BF16)
            St = sb("St", [128, 4, 128], F32)
            sc = {n: sb("sc_" + n, [128, NCH], F32) for n in ("dl", "ea", "eb", "er")}
            oT = sb("oT", [128, 8, T], BF16)
            qZ = sb("qZ", [128, 8, T], BF16)
            Kc = sb("Kc", [128, 4, NSLOT * 128], BF16)
            Vc = sb("Vc", [128, NSLOT, 4, 192], BF16)
            Eb = [sb("Eb%d" % i, [128, 512], BF16) for i in range(4)]
            rden = [sb("rden%d" % i, [128, T], F32) for i in range(2)]

            S.op("dve", lambda e: e.memset(St[:], 0.0), w=["St"])
            S.op("dve", lambda e: e.memset(ktokE[:], 0.0), w=["ktok"])
            S.op("dve", lambda e: e.memset(ktokO[:], 0.0), w=["ktok"])
            S.op("dve", lambda e: e.memset(qZ[:], 0.0), w=[("qZ", p) for p in range(4)])
            S.op("dve", lambda e: e.memset(Kc[:], 0.0), w=[("kc", s) for s in range(NSLOT)])
            S.op("dve", lambda e: e.memset(Vc[:], 0.0), w=[("vc", s) for s in range(NSLOT)])

            lbv, omlv = self.lb[:, l, :], self.oml[:, l, :]
            nw = self.vsb["hgrn_norm_w"][:, l, :]
            ps = self.ps
            dcount = [0]

            def dense_fm(col0, xt, xres, consume):
                bi = dcount[0] % 2
                dcount[0] += 1
                bank = ps[bi]
                for k in range(8):
                    S.op("pe", lambda e: e.matmul(bank[:, 0:T], Win[:, k, col0:col0 + 128], xt[:, k, :],
                                                  start=(k == 0), stop=(k == 7)),
                         r=[("Win", k), xres], w=["ps%d" % bi])
                consume(bank[:, 0:T], "ps%d" % bi)

            s_i = [0]
            own_blk = cfg["own_start"] * NBLK
            for t in range(cfg["ntiles"]):
                kv_only = t < cfg["kv_only"]
                pre_own = t < cfg["own_start"]
                xt = xTb[t % NXB]
                xres = ("xTb", t % NXB)
                S.dma("pool", xt[:], src[:, :, t * T:(t + 1) * T], w=[xres])
                if t == cfg["own_start"]:
                    S.op("dve", lambda e: e.tensor_scalar(out=St[:], in0=St[:], scalar1=self.flag[:, 0:1],
                                                          scalar2=None, op0=ALU.mult), r=["flag"], w=["St"])
                slot0 = (t * NBLK) % NSLOT
                kres = [("kc", slot0 + bb) for bb in range(NBLK)]
                for p in range(4):
                    if not kv_only:
                        def cqa(ap, rn, p=p):
                            S.op("act", lambda e: e.activation(out=qZ[0:64, 2 * p, :], in_=ap[0:64, :], func=AF.Copy,
                                                               scale=0.125), w=[rn, ("qZ", p)])
                            S.op("act", lambda e: e.activation(out=qZ[64:128, 2 * p + 1, :], in_=ap[64:128, :],
                                                               func=AF.Copy, scale=0.125), w=[rn, ("qZ", p)])
                        dense_fm(2048 + p * 128, xt, xres, cqa)
                    dense_fm(2560 + p * 128, xt, xres, lambda ap, rn, p=p: S.op(
                        "dve", lambda e: e.tensor_copy(out=Kc[:, p, slot0 * 128:slot0 * 128 + T], in_=ap),
                        w=[rn] + kres))
                for bb in range(NBLK):
                    slot = slot0 + bb
                    vb = 2 if bb % 2 == 0 else 7
                    for k in range(8):
                        S.op("pe", lambda e: e.matmul(ps[vb][:, :], xt[:, k, bb * 128:(bb + 1) * 128],
                                                      Win[:, k, 3072:3584], start=(k == 0), stop=(k == 7)),
                             r=[("Win", k), xres], w=["ps%d" % vb])
                    vps = ps[vb][:, :].rearrange("p (a b) -> p a b", b=128)
                    if pre_own:
                        for (o0, i0) in ((0, 0), (128, 64)):
                            S.op("dve", lambda e: e.tensor_scalar(out=Vc[:, slot, :, o0:o0 + 64], in0=vps[:, :, i0:i0 + 64],
                                                                  scalar1=self.flag[:, 0:1], scalar2=None, op0=ALU.mult),
                                 r=["flag"], w=["ps%d" % vb, ("vc", slot)])
                        S.op("dve", lambda e: e.tensor_copy(out=Vc[:, slot, :, 64:128],
                                                            in_=self.flag[:, 0:1].to_broadcast([128, 4, 64])),
                             r=["flag"], w=[("vc", slot)])
                    else:
                        S.op("act", lambda e: e.activation(out=Vc[:, slot, :, 0:64], in_=vps[:, :, 0:64], func=AF.Copy),
                             w=["ps%d" % vb, ("vc", slot)])
                        S.op("dve", lambda e: e.tensor_copy(out=Vc[:, slot, :, 128:192], in_=vps[:, :, 64:128]),
                             w=["ps%d" % vb, ("vc", slot)])
                        S.op("dve", lambda e: e.memset(Vc[:, slot, :, 64:128], 1.0), w=[("vc", slot)])
                for bb in range(NBLK):
                    vb = 2 if bb % 2 == 0 else 7
                    for k in range(8):
                        S.op("pe", lambda e: e.matmul(ps[vb][:, :], xt[:, k, bb * 128:(bb + 1) * 128],
                                                      Win[:, k, 1024:1536], start=(k == 0), stop=(k == 7)),
                             r=[("Win", k), xres], w=["ps%d" % vb])
                    S.op("act", lambda e: e.activation(out=vtok[:, bb, :], in_=ps[vb][:, :], func=AF.Copy),
                         w=["ps%d" % vb, "vtok"])
                A, Q, B, Kk, G, Dm, Ex = (hf[n] for n in ("A", "Q", "B", "Kk", "G", "Dm", "Ex"))
                def hg_front(h):
                    def cf(ap, rn):
                        S.op("act", lambda e: e.activation(out=B[:], in_=ap, func=AF.Exp, scale=-1.0),
                             w=[rn, "h_B"])
                        S.op("act", lambda e: e.activation(out=B[:], in_=B[:], func=AF.Ln, bias=1.0),
                             w=["h_B"])
                        S.op("act", lambda e: e.activation(out=B[:], in_=B[:], func=AF.Exp, scale=-1.0),
                             w=["h_B"])
                        S.op("dve", lambda e: e.tensor_scalar(out=B[:], in0=B[:], scalar1=omlv[:, h:h + 1],
                                                              scalar2=lbv[:, h:h + 1], op0=ALU.mult, op1=ALU.add),
                             r=["lb", "oml"], w=["h_B"])
                    dense_fm(512 + h * 128, xt, xres, cf)
                    if not kv_only:
                        def cq(ap, rn):
                            S.op("act", lambda e: e.activation(out=A[:], in_=ap, func=AF.Exp, scale=-1.0),
                                 w=[rn, "h_A"])
                            S.op("act", lambda e: e.activation(out=A[:], in_=A[:], func=AF.Ln, bias=1.0),
                                 w=["h_A"])
                            S.op("act", lambda e: e.activation(out=A[:], in_=A[:], func=AF.Exp, scale=-1.0),
                                 w=["h_A"])
                            S.op("dve", lambda e: e.tensor_tensor(out=Q[:], in0=ap, in1=A[:], op=ALU.mult),
                                 r=["h_A"], w=[rn, "h_Q"])
                        dense_fm(h * 128, xt, xres, cq)

                        def cg(ap, rn):
                            S.op("act", lambda e: e.activation(out=Sg2[h % 2][:], in_=ap, func=AF.Exp, scale=-1.0),
                                 w=[rn, ("h_Sg", h % 2)])
                            S.op("act", lambda e: e.activation(out=Sg2[h % 2][:], in_=Sg2[h % 2][:], func=AF.Ln, bias=1.0),
                                 w=[("h_Sg", h % 2)])
                            S.op("act", lambda e: e.activation(out=Sg2[h % 2][:], in_=Sg2[h % 2][:], func=AF.Exp, scale=-1.0),
                                 w=[("h_Sg", h % 2)])
                        dense_fm(1536 + h * 128, xt, xres, cg)

                    S.op("dve", lambda e: e.tensor_scalar(out=Kk[:], in0=B[:], scalar1=-1.0, scalar2=1.0,
                                                          op0=ALU.mult, op1=ALU.add), r=["h_B"], w=["h_Kk"])
                    S.op("act", lambda e: e.activation(out=B[:], in_=B[:], func=AF.Ln), w=["h_B"])
                    S.op("dve", lambda e: e.tensor_tensor_scan(out=G[:], data0=self.reset[:], data1=B[:],
                                                               initial=0.0, op0=ALU.mult, op1=ALU.add),
                         r=["reset", "h_B"], w=["h_G"])
                    Gv = G[:].rearrange("p (c t) -> p c t", t=64)
                    Dv = Dm[:].rearrange("p (c t) -> p c t", t=64)
                    S.op("dve", lambda e: e.tensor_tensor(out=Dv, in0=Gv,
                                                          in1=Gv[:, :, REF:REF + 1].to_broadcast([128, NCH, 64]),
                                                          op=ALU.subtract), r=["h_G"], w=["h_Dm"])
                    S.op("act", lambda e: e.activation(out=Ex[:], in_=Dm[:], func=AF.Exp, scale=-1.0),
                         r=["h_Dm"], w=["h_Ex"])
                    S.op("dve", lambda e: e.tensor_tensor(out=kt[:], in0=Kk[:], in1=Ex[:], op=ALU.mult),
                         r=["h_Kk", "h_Ex"], w=["kt"])
                    if not kv_only:
                        S.op("act", lambda e: e.activation(out=Ex[:], in_=Dm[:], func=AF.Exp),
                             r=["h_Dm"], w=["h_Ex"])
                        S.op("dve", lambda e: e.tensor_tensor(out=qt2[h % 2][:], in0=Q[:], in1=Ex[:], op=ALU.mult),
                             r=["h_Q", "h_Ex"], w=[("qt", h % 2)])
                    S.op("dve", lambda e: e.tensor_tensor(out=sc["dl"][:], in0=Gv[:, :, 63], in1=Gv[:, :, REF],
                                                          op=ALU.subtract), r=["h_G"], w=["sc_dl"])
                    S.op("act", lambda e: e.activation(out=sc["ea"][:], in_=Gv[:, :, 63], func=AF.Exp),
                         r=["h_G"], w=["sc_ea"])
                    S.op("act", lambda e: e.activation(out=sc["eb"][:], in_=sc["dl"][:], func=AF.Exp),
                         r=["sc_dl"], w=["sc_eb"])
                    S.op("act", lambda e: e.activation(out=sc["er"][:], in_=Gv[:, :, REF], func=AF.Exp),
                         r=["h_G"], w=["sc_er"])
                def hg_back1(h):
                    for bb in range(NBLK):
                        S.op("pe", lambda e: e.matmul(ps[2][:, bb * 128:(bb + 1) * 128],
                                                      kt[:, bb * 128:(bb + 1) * 128], self.ident[:],
                                                      start=True, stop=True),
                             r=["kt", "ident"], w=["ps2"])
                    S.op("act", lambda e: e.activation(out=ktokE[0:64, :, :].rearrange("p c k -> p (c k)"),
                                                       in_=ps[2][0:64, :], func=AF.Copy), w=["ps2", "ktok"])
                    S.op("act", lambda e: e.activation(out=ktokO[64:128, :, :].rearrange("p c k -> p (c k)"),
                                                       in_=ps[2][64:128, :], func=AF.Copy), w=["ps2", "ktok"])
                    if not kv_only:
                        for bb in range(NBLK):
                            S.op("pe", lambda e: e.matmul(ps[7][:, bb * 128:(bb + 1) * 128],
                                                          kt[:, bb * 128:(bb + 1) * 128],
                                                          qt2[h % 2][:, bb * 128:(bb + 1) * 128],
                                                          start=True, stop=True),
                                 r=["kt", ("qt", h % 2)], w=["ps7"])
                        S.op("dve", lambda e: e.tensor_tensor(
                            out=ATs[:].rearrange("p (c t) -> p c t", t=128),
                            in0=ps[7][:, 0:T].rearrange("p (c t) -> p c t", t=128),
                            in1=self.tri[:].rearrange("p (o t) -> p o t", o=1).to_broadcast([128, NBLK, 128]),
                            op=ALU.mult), r=["tri"], w=["ps7", "ATs"])
                    for c in range(NCH):
                        ub = 3 + c // 4
                        kz = ktokE if c % 2 == 0 else ktokO
                        S.op("pe", lambda e: e.matmul(ps[ub][:, (c % 4) * 128:(c % 4 + 1) * 128], kz[:, c // 2, :],
                                                      vtok[:, c // 2, h * 128:(h + 1) * 128], start=True, stop=True),
                             r=["ktok", "vtok"], w=["ps%d" % ub])
                    for c in range(NCH):
                        ub = 3 + c // 4
                        if not kv_only:
                            S.op("dve", lambda e: e.tensor_scalar(out=Sp[:, c, :], in0=St[:, h, :],
                                                                  scalar1=sc["er"][:, c:c + 1], scalar2=None,
                                                                  op0=ALU.mult), r=["St", "sc_er"], w=["Sp"])
                        S.op("dve", lambda e: e.tensor_scalar(out=tmpU[:], in0=ps[ub][:, (c % 4) * 128:(c % 4 + 1) * 128],
                                                              scalar1=sc["eb"][:, c:c + 1], scalar2=None,
                                                              op0=ALU.mult), r=["sc_eb"], w=["ps%d" % ub, "tmpU"])
                        S.op("dve", lambda e: e.scalar_tensor_tensor(out=St[:, h, :], in0=St[:, h, :],
                                                                     scalar=sc["ea"][:, c:c + 1], in1=tmpU[:],
                                                                     op0=ALU.mult, op1=ALU.add),
                             r=["sc_ea", "tmpU"], w=["St"])
                def hg_back2(h):
                    if not kv_only:
                        for bb in range(NBLK):
                            S.op("pe", lambda e: e.matmul(ps[5][:, bb * 128:(bb + 1) * 128],
                                                          vtok[:, bb, h * 128:(h + 1) * 128],
                                                          ATs[:, bb * 128:(bb + 1) * 128], start=True, stop=False),
                                 r=["vtok", "ATs"], w=["ps5"])
                            for c in (2 * bb, 2 * bb + 1):
                                S.op("pe", lambda e: e.matmul(ps[5][:, c * 64:(c + 1) * 64], Sp[:, c, :],
                                                              qt2[h % 2][:, c * 64:(c + 1) * 64], start=False,
                                                              stop=(c == 2 * bb + 1)),
                                     r=["Sp", ("qt", h % 2)], w=["ps5"])
                        S.op("act", lambda e: e.activation(out=A[:], in_=ps[5][:, :], func=AF.Square),
                             w=["ps5", "h_A"])
                        S.op("pe", lambda e: e.matmul(ps[6][:, :], self.avg128[:], A[:], start=True, stop=True),
                             r=["h_A", "avg128"], w=["ps6"])
                        S.op("dve", lambda e: e.tensor_scalar(out=T1[:], in0=ps[6][:, :], scalar1=RMS_EPS,
                                                              scalar2=None, op0=ALU.add), w=["ps6", "h_T1"])
                        S.op("act", lambda e: e.activation(out=T1[:], in_=T1[:], func=AF.Ln), w=["h_T1"])
                        S.op("act", lambda e: e.activation(out=T1[:], in_=T1[:], func=AF.Exp, scale=-0.5), w=["h_T1"])
                        S.op("dve", lambda e: e.tensor_tensor(out=T1[:], in0=ps[5][:, :], in1=T1[:], op=ALU.mult),
                             w=["ps5", "h_T1"])
                        S.op("dve", lambda e: e.scalar_tensor_tensor(out=oT[:, h, :], in0=T1[:],
                                                                     scalar=nw[:, h:h + 1], in1=Sg2[h % 2][:],
                                                                     op0=ALU.mult, op1=ALU.mult),
                             r=["h_T1", ("h_Sg", h % 2), "v_hgrn_norm_w"], w=[("oT", h)])
                hg_front(0)
                if not kv_only:
                    qb0 = t * NBLK
                    items = []
                    for hh in range(8):
                        ds = [d for d in range(NBLK + 16) if qb0 + NBLK - 1 - d >= 0]
                        ds = [d for d in ds if NBLK - 1 <= d <= 16] + [d for d in ds if d < NBLK - 1 or d > 16]
                        for i, d in enumerate(ds):
                            items.append((hh, i, d, i == len(ds) - 1))
                    LAG = 2
                    STEP = 2
                    sbanks = (3, 4, 0, 1)
                    bufs = {}

                    def cols(d):
                        qlo = max(0, NBLK - 1 - d)
                        qhi = min(NBLK - 1, NBLK - 1 + 16 - d)
                        return qlo * 128, (qhi + 1) * 128

                    def emit_qk(n):
                        hh, i, d, last = items[n]
                        c0, c1 = cols(d)
                        p, half = hh // 2, hh % 2
                        pr = slice(half * 64, half * 64 + 64)
                        ks = (qb0 + NBLK - 1 - d) % NSLOT
                        bi = s_i[0] % 4
                        s_i[0] += 1
                        bufs[n] = bi
                        sbk = sbanks[bi]
                        S.op("pe", lambda e: e.matmul(ps[sbk][:, c0:c1], Kc[:, p, ks * 128:(ks + 1) * 128],
                                                      qZ[:, hh, c0:c1], start=True, stop=True),
                             r=[("kc", ks), ("qZ", p)], w=["ps%d" % sbk])
                        S.op("act", lambda e: e.activation(out=Eb[bi][:, c0:c1], in_=ps[sbk][:, c0:c1], func=AF.Exp),
                             w=["ps%d" % sbk, ("Eb", bi)])
                        mk = self.amask[:, d:d + NBLK, :].rearrange("p a b -> p (a b)")
                        S.op("dve", lambda e: e.tensor_tensor(out=Eb[bi][:, c0:c1], in0=Eb[bi][:, c0:c1],
                                                              in1=mk[:, c0:c1], op=ALU.mult),
                             r=["amask"], w=[("Eb", bi)])

                    dly = min(8, len(items) // 8)

                    def emit_pv(n):
                        hh, i, d, last = items[n]
                        c0, c1 = cols(d)
                        p, half = hh // 2, hh % 2
                        pr = slice(half * 64, half * 64 + 64)
                        ks = (qb0 + NBLK - 1 - d) % NSLOT
                        bi = bufs.pop(n)
                        bn = (5, 6, 7, 2)[hh % 4]
                        lw = Vc[:, ks, p, 0:128] if half == 0 else Vc[:, ks, p, 64:192]
                        S.op("pe", lambda e: e.matmul(ps[bn][:, c0:c1], lw, Eb[bi][:, c0:c1],
                                                      start=(i == 0), stop=last),
                             r=[("vc", ks), ("Eb", bi)], w=["ps%d" % bn])
                        if last:
                            dpr = slice(64, 128) if half == 0 else slice(0, 64)
                            rd, rd2 = rden[0], rden[1]

                            def fin(bn=bn, pr=pr, p=p, half=half):
                                S.op("dve", lambda e: e.tensor_tensor(out=oT[pr, 4 + p, :], in0=ps[bn][pr, :],
                                                                      in1=rd2[pr, :], op=ALU.mult),
                                     r=[("rd2", half)], w=["ps%d" % bn, ("oT", 4 + p)])

                            def chain(bn=bn, pr=pr, dpr=dpr, half=half, n=n, fin=fin):
                                for it in [x for x in deferred if x[2] == "fin"]:
                                    deferred.remove(it)
                                    it[1]()
                                S.op("act", lambda e: e.activation(out=rd[dpr, :], in_=ps[bn][dpr, :], func=AF.Ln,
                                                                   bias=1e-30),
                                     w=["ps%d" % bn, ("rd", half)])
                                S.op("act", lambda e: e.activation(out=rd[dpr, :], in_=rd[dpr, :], func=AF.Exp,
                                                                   scale=-1.0), w=[("rd", half)])
                                S.dma("sp", rd2[pr, :], rd[dpr, :], r=[("rd", half)], w=[("rd2", half)])
                                deferred.append((n + 2 * dly, fin, "fin"))
                            deferred.append((n + dly, chain, "chain"))

                    deferred = []
                    nsteps = (len(items) + STEP - 1) // STEP
                    for st in range(nsteps + 1):
                        for n in range(st * STEP, min((st + 1) * STEP, len(items))):
                            emit_qk(n)
                        if st >= 1:
                            for n in range((st - 1) * STEP, min(st * STEP, len(items))):
                                emit_pv(n)
                        for it in [x for x in deferred if x[0] <= st * STEP]:
                            if it in deferred:
                                deferred.remove(it)
                                it[1]()
                    while deferred:
                        deferred.pop(0)[1]()
                for h in range(4):
                    hg_back1(h)
                    if h < 3:
                        hg_front(h + 1)
                    hg_back2(h)
                if kv_only:
                    continue
                tf = t - cfg["kv_only"]
                for m in range(8):
                    bi = dcount[0] % 2
                    dcount[0] += 1
                    bank = ps[bi]
                    for j in range(8):
                        S.op("pe", lambda e: e.matmul(bank[:, 0:T], Wo[:, j, m * 128:(m + 1) * 128], oT[:, j, :],
                                                      start=(j == 0), stop=(j == 7)),
                             r=["Wo", ("oT", j)], w=["ps%d" % bi])
                    xrb, zb = xr[m % 2], zr[m % 2]
                    S.dma("sp", xrb[:], src[:, m, t * T:(t + 1) * T], w=[("xr", m % 2)])
                    S.op("dve", lambda e: e.scalar_tensor_tensor(out=zb[:], in0=xrb[:], scalar=ALPHA,
                                                                 in1=bank[:, 0:T], op0=ALU.mult, op1=ALU.add),
                         r=[("xr", m % 2)], w=["ps%d" % bi, ("zr", m % 2)])
                    S.dma("sp", x1a[:, m, tf * T:(tf + 1) * T], zb[:], r=[("zr", m % 2)], w=[("x1a", tf)])

    def phase_b(self, cfg, x1a, dst, nfull):
        nc, S = self.nc, self.S
        l = cfg["l"]
        ps = self.ps
        with ExitStack() as es:
            sb = lambda n, s, d: self.sb(n + "_b%d" % l, s, d, es)
            W1 = sb("W1", [128, 8, 4096], BF16)
            W2 = sb("W2", [128, 32, D], BF16)
            for k in range(8):
                S.dma("pool", W1[:, k, :], self.w_ff1[l, k * 128:(k + 1) * 128, :], w=[("W1", k)])
            for k in range(32):
                S.dma("pool", W2[:, k, :], self.w_ff2[l, k * 128:(k + 1) * 128, :], w=[("W2", k)])
            xfb = [sb("x1f%d" % i, [128, 8, T], F32) for i in range(2)]
            x1b = sb("x1b", [128, 8, T], BF16)
            rr = [sb("rr%d" % i, [128, T], BF16) for i in range(2)]
            self.ln_zs = [sb("lnzs%d" % i, [128, T], BF16) for i in range(2)]
            self.ln_zc = [sb("lnzc%d" % i, [128, T], BF16) for i in range(2)]
            NH = 16
            hT = sb("hT", [128, NH, T], BF16)
            self.lnring = [sb("lnrb%d" % i, [128, T], F32) for i in range(2)]
            self.ln_mean = sb("ln_meanb", [128, T], F32)
            self.ln_rstd = sb("ln_rstdb", [128, T], F32)
            outr = self.lnring
            g1, b1 = self.vsb["ln1_g"][:, l, :], self.vsb["ln1_b"][:, l, :]
            g2, b2 = self.vsb["ln2_g"][:, l, :], self.vsb["ln2_b"][:, l, :]

            def load(tf):
                xf = xfb[tf % 2]
                for m in range(8):
                    S.dma("sp", xf[:, m, :], x1a[:, m, tf * T:(tf + 1) * T], r=[("x1a", tf)], w=[("x1f", tf % 2, m)])

            def ln1(tf, part="all"):
                xf = xfb[tf % 2]
                zres = lambda m: ("x1f", tf % 2, m)

                def emit1(m):
                    S.op("act", lambda e: e.activation(out=x1b[:, m, :], in_=xf[:, m, :], func=AF.Identity,
                                                       scale=g1[:, m:m + 1], bias=b1[:, m:m + 1]),
                         r=["v_ln1_g", "v_ln1_b", zres(m)], w=[("x1b", m)])
                    S.op("act", lambda e: e.activation(out=xf[:, m, :], in_=xf[:, m, :], func=AF.Identity,
                                                       scale=g1[:, m:m + 1], bias=b1[:, m:m + 1]),
                         r=["v_ln1_g", "v_ln1_b"], w=[zres(m)])
                self.layer_norm(xf, zres, emit1, part)

            load(0)
            ln1(0)
            for tf in range(nfull):
                xf = xfb[tf % 2]
                zres = lambda m: ("x1f", tf % 2, m)
                if tf + 1 < nfull:
                    load(tf + 1)
                for half in range(2):
                    for jj in range(NH):
                        j = half * NH + jj
                        bi = 2 + (j % 2)
                        for k in range(8):
                            S.op("pe", lambda e: e.matmul(ps[bi][:, 0:T], W1[:, k, j * 128:(j + 1) * 128],
                                                          x1b[:, k, :], start=(k == 0), stop=(k == 7)),
                                 r=[("W1", k), ("x1b", k)], w=["ps%d" % bi])
                        rb = rr[j % 2]
                        S.op("act", lambda e: e.activation(out=rb[:], in_=ps[bi][:, 0:T], func=AF.Relu),
                             w=["ps%d" % bi, ("rr", j % 2)])
                        S.op("dve", lambda e: e.scalar_tensor_tensor(out=hT[:, jj, :], in0=ps[bi][:, 0:T], scalar=0.0,
                                                                     in1=rb[:], op0=ALU.max, op1=ALU.mult),
                             r=[("rr", j % 2)], w=["ps%d" % bi, ("hT", jj)])
                    if tf + 1 < nfull:
                        ln1(tf + 1, "stats" if half == 0 else "apply")
                    for m in range(8):
                        bi = 4 + (m % 2)
                        for jj in range(NH):
                            j = half * NH + jj
                            S.op("pe", lambda e: e.matmul(ps[bi][:, 0:T], W2[:, j, m * 128:(m + 1) * 128],
                                                          hT[:, jj, :], start=(jj == 0), stop=(jj == NH - 1)),
                                 r=[("W2", j), ("hT", jj)], w=["ps%d" % bi])
                        if half == 0:
                            S.op("dve", lambda e: e.scalar_tensor_tensor(out=xf[:, m, :], in0=xf[:, m, :],
                                                                         scalar=ALPHA, in1=ps[bi][:, 0:T],
                                                                         op0=ALU.mult, op1=ALU.add),
                                 w=["ps%d" % bi, zres(m)])
                        else:
                            S.op("dve", lambda e: e.tensor_tensor(out=xf[:, m, :], in0=xf[:, m, :],
                                                                  in1=ps[bi][:, 0:T], op=ALU.add),
                                 w=["ps%d" % bi, zres(m)])

                def emit2(m):
                    ob = outr[m % 2]
                    S.op("act", lambda e: e.activation(out=ob[:], in_=xf[:, m, :], func=AF.Identity,
                                                       scale=g2[:, m:m + 1], bias=b2[:, m:m + 1]),
                         r=[zres(m), "v_ln2_g", "v_ln2_b"], w=[("lnr", m % 2)])
                    S.dma("sp", dst[:, m, tf * T:(tf + 1) * T], ob[:], r=[("lnr", m % 2)],
                          w=[("dst", id(dst), tf, m)])
                self.layer_norm(xf, zres, emit2)


VEC_NAMES = ("lower_bounds", "hgrn_norm_w", "ln1_g", "ln1_b", "ln2_g", "ln2_b")


def _common_inputs(inp):
    m = dict(host_consts())
    for nm in ("w_in", "w_out", "w_ff1", "w_ff2"):
        m[nm] = np.ascontiguousarray(inp[nm], dtype=np.float32)
    for nm in VEC_NAMES:
        v = np.asarray(inp[nm], dtype=np.float32)
        m[nm] = np.ascontiguousarray(v.reshape(2, v.shape[1] // 128, 128).transpose(0, 2, 1))
    return m


_NC_CACHE = {}


def _fused_nc():
    if "fused" not in _NC_CACHE:
        cfgs = [dict(l=0, ntiles=16, kv_only=4, own_start=8), dict(l=1, ntiles=12, kv_only=4, own_start=4)]
        _NC_CACHE["fused"] = Builder(8192, cfgs, 4096).build()
    return _NC_CACHE["fused"]


def kernel(x, w_in, w_out, lower_bounds, hgrn_norm_w, ln1_g, ln1_b, w_ff1, w_ff2, ln2_g, ln2_b):
    inp = dict(w_in=w_in, w_out=w_out, lower_bounds=lower_bounds, hgrn_norm_w=hgrn_norm_w, ln1_g=ln1_g,
               ln1_b=ln1_b, w_ff1=w_ff1, w_ff2=w_ff2, ln2_g=ln2_g, ln2_b=ln2_b)
    common = _common_inputs(inp)
    x = np.asarray(x, dtype=np.float32)
    B, Sq, _ = x.shape
    OWN, HALO = 4096, 4096
    in_maps = []
    for c in range(8):
        b, q = c // 4, c % 4
        st = q * OWN
        xs = np.zeros((HALO + OWN, D), np.float32)
        lo = st - HALO
        if lo < 0:
            xs[-lo:] = x[b, 0:st + OWN]
        else:
            xs[:] = x[b, lo:st + OWN]
        m = dict(common)
        m["xT"] = np.ascontiguousarray(xs.T)
        m["flag"] = np.full((128, 1), 0.0 if q == 0 else 1.0, np.float32)
        in_maps.append(m)
    nc = _fused_nc()
    res = run_bass_kernel_spmd(nc, in_maps, core_ids=list(range(8)))
    out = np.empty((B, Sq, D), np.float32)
    for c in range(8):
        b, q = c // 4, c % 4
        out[b, q * OWN:(q + 1) * OWN] = res.results[c]["outT"].T
    return out
```
